# Optimizing a Trainium2 kernel written in Bass

```python
import math
import jax, jax.numpy as jnp
from jax import lax
import numpy as np

D_MODEL = 2048
BATCH = 8
SEQ = 2048
DEPTH = 2

N_A_LAYERS = DEPTH // 2
N_B_LAYERS = DEPTH - N_A_LAYERS

CONV_WIDTH = 31

HEAD_DIM = 128
N_HEADS = D_MODEL // HEAD_DIM
N_KV_HEADS = 4
GROUP = N_HEADS // N_KV_HEADS
WINDOWS = (128, 512, 2048)
DILATIONS = (1, 4, 16)
BLOCK = 128
PAD_UNIT = math.lcm(*DILATIONS) * BLOCK

ROPE_THETA = 500000.0
ROT_DIM = HEAD_DIM // 4

D_FF = 4 * D_MODEL

NORM_EPS = 1e-6
LN_EPS = 1e-5

kernel_name = "yoco_conformer_dilated_hybrid"


def rmsnorm(x, g):
    x32 = x.astype(jnp.float32)
    y = x32 * lax.rsqrt(jnp.mean(x32 * x32, axis=-1, keepdims=True) + NORM_EPS)
    return (y * g.astype(jnp.float32)).astype(x.dtype)


def layernorm(x, g, b):
    x32 = x.astype(jnp.float32)
    mu = jnp.mean(x32, axis=-1, keepdims=True)
    xc = x32 - mu
    var = jnp.mean(xc * xc, axis=-1, keepdims=True)
    y = xc * lax.rsqrt(var + LN_EPS) * g.astype(jnp.float32) + b.astype(jnp.float32)
    return y.astype(x.dtype)


def rope_tables(seq):
    pos = jnp.arange(seq, dtype=jnp.float32)
    inv = ROPE_THETA ** (-jnp.arange(0, ROT_DIM, 2, dtype=jnp.float32) / ROT_DIM)
    ang = pos[:, None] * inv[None, :]
    return jnp.cos(ang), jnp.sin(ang)


def partial_rope(x, cos, sin):
    half = ROT_DIM // 2
    shape = (1, x.shape[1]) + (1,) * (x.ndim - 3) + (half,)
    c = cos.reshape(shape).astype(x.dtype)
    s = sin.reshape(shape).astype(x.dtype)
    x1 = x[..., :half]
    x2 = x[..., half:ROT_DIM]
    return jnp.concatenate([x1 * c - x2 * s, x2 * c + x1 * s, x[..., ROT_DIM:]], axis=-1)


def conformer_conv(y, w_in, b_in, w_dw, b_dw, ln_g, ln_b, w_out, b_out):
    u = y @ w_in + b_in
    a, gate = jnp.split(u, 2, axis=-1)
    u = a * jax.nn.sigmoid(gate)
    u = lax.conv_general_dilated(
        u, w_dw[:, None, :], window_strides=(1,), padding=[(CONV_WIDTH - 1, 0)],
        dimension_numbers=("NWC", "WIO", "NWC"), feature_group_count=u.shape[-1]) + b_dw
    u = jax.nn.silu(layernorm(u, ln_g, ln_b))
    return u @ w_out + b_out


def sq_relu_mlp(y, w_in, w_out):
    h = jax.nn.relu(y @ w_in)
    return (h * h) @ w_out


def shared_kv(h, g, w_kv, cos, sin):
    B, S, _ = h.shape
    kv = rmsnorm(h, g) @ w_kv
    k, v = jnp.split(kv, 2, axis=-1)
    k = partial_rope(k.reshape(B, S, N_KV_HEADS, HEAD_DIM), cos, sin)
    v = v.reshape(B, S, N_KV_HEADS, HEAD_DIM)
    return k, v


def _with_prev_block(t):
    prev = jnp.pad(t[:, :-1], ((0, 0), (1, 0)) + ((0, 0),) * (t.ndim - 2))
    return jnp.concatenate([prev, t], axis=2)


def dilated_branch(q, k, v, dil, steps):
    B, Sp, KVH, G, hd = q.shape
    n = Sp // dil // BLOCK
    qb = q.reshape(B, n, BLOCK, dil, KVH, G, hd)
    kw = _with_prev_block(k.reshape(B, n, BLOCK, dil, KVH, hd))
    vw = _with_prev_block(v.reshape(B, n, BLOCK, dil, KVH, hd))
    s = jnp.einsum("bncrhgd,bnkrhd->bnrhgck", qb, kw,
                   preferred_element_type=jnp.float32) * (1.0 / math.sqrt(hd))
    c = jnp.arange(BLOCK)[:, None]
    j = jnp.arange(2 * BLOCK)[None, :]
    dist = c + BLOCK - j
    band = (dist >= 0) & (dist <= steps)
    first = (jnp.arange(n) == 0)[:, None, None]
    valid = band[None] & ~(first & (j < BLOCK)[None])
    s = jnp.where(valid[None, :, None, None, None], s, -jnp.inf)
    m = jnp.max(s, axis=-1, keepdims=True)
    p = jnp.exp(s - m)
    l = jnp.sum(p, axis=-1, keepdims=True)
    o = jnp.einsum("bnrhgck,bnkrhd->bncrhgd", p.astype(v.dtype), vw)
    l_t = jnp.transpose(l[..., 0], (0, 1, 5, 2, 3, 4))
    lse = jnp.transpose((m + jnp.log(l))[..., 0], (0, 1, 5, 2, 3, 4))
    o = o.astype(jnp.float32) / l_t[..., None]
    return o.reshape(B, Sp, KVH, G, hd), lse.reshape(B, Sp, KVH, G)


def dilated_mixture(q, k, v):
    S = q.shape[1]
    Sp = -(-S // PAD_UNIT) * PAD_UNIT
    pad = Sp - S
    q = jnp.pad(q, ((0, 0), (0, pad), (0, 0), (0, 0), (0, 0)))
    k = jnp.pad(k, ((0, 0), (0, pad), (0, 0), (0, 0)))
    v = jnp.pad(v, ((0, 0), (0, pad), (0, 0), (0, 0)))
    outs, lses = [], []
    for win, dil in zip(WINDOWS, DILATIONS):
        o, lse = dilated_branch(q, k, v, dil, win // dil)
        outs.append(o)
        lses.append(lse)
    w = jax.nn.softmax(jnp.stack(lses, axis=0), axis=0)
    o = jnp.einsum("ibshg,ibshgd->bshgd", w, jnp.stack(outs, axis=0))
    return o[:, :S].astype(q.dtype)


def dilated_attn_layer(y, k, v, w_q, w_o, cos, sin):
    B, S, _ = y.shape
    q = partial_rope((y @ w_q).reshape(B, S, N_KV_HEADS, GROUP, HEAD_DIM), cos, sin)
    o = dilated_mixture(q, k, v)
    return o.reshape(B, S, N_HEADS * HEAD_DIM) @ w_o


def setup_inputs(seed: int = 0) -> dict:
    key = jax.random.key(seed)
    ks = jax.random.split(key, 20)
    D = D_MODEL
    f32 = jnp.float32

    def nrm(k, shape, scale):
        return jax.random.normal(k, shape, f32) * scale

    return {
        "x": nrm(ks[0], (BATCH, SEQ, D), 1.0),
        "norm_mix": 1.0 + nrm(ks[1], (DEPTH, D), 0.01),
        "norm_mlp": 1.0 + nrm(ks[2], (DEPTH, D), 0.01),
        "conv_w_in": nrm(ks[3], (N_A_LAYERS, D, 2 * D), D ** -0.5),
        "conv_b_in": nrm(ks[4], (N_A_LAYERS, 2 * D), 0.02),
        "conv_w_dw": nrm(ks[5], (N_A_LAYERS, CONV_WIDTH, D), CONV_WIDTH ** -0.5),
        "conv_b_dw": nrm(ks[6], (N_A_LAYERS, D), 0.02),
        "conv_ln_g": 1.0 + nrm(ks[7], (N_A_LAYERS, D), 0.01),
        "conv_ln_b": nrm(ks[8], (N_A_LAYERS, D), 0.02),
        "conv_w_out": nrm(ks[9], (N_A_LAYERS, D, D), D ** -0.5),
        "conv_b_out": nrm(ks[10], (N_A_LAYERS, D), 0.02),
        "kv_norm": 1.0 + nrm(ks[11], (D,), 0.01),
        "w_kv": nrm(ks[12], (D, 2 * N_KV_HEADS * HEAD_DIM), D ** -0.5),
        "attn_w_q": nrm(ks[13], (N_B_LAYERS, D, N_HEADS * HEAD_DIM), D ** -0.5),
        "attn_w_o": nrm(ks[14], (N_B_LAYERS, N_HEADS * HEAD_DIM, D), (N_HEADS * HEAD_DIM) ** -0.5),
        "mlp_w_in": nrm(ks[15], (DEPTH, D, D_FF), D ** -0.5),
        "mlp_w_out": nrm(ks[16], (DEPTH, D_FF, D), D_FF ** -0.5),
        "final_norm": 1.0 + nrm(ks[17], (D,), 0.01),
    }


def reference(x, norm_mix, norm_mlp, conv_w_in, conv_b_in, conv_w_dw, conv_b_dw,
              conv_ln_g, conv_ln_b, conv_w_out, conv_b_out, kv_norm, w_kv,
              attn_w_q, attn_w_o, mlp_w_in, mlp_w_out, final_norm):
    S = x.shape[1]
    cos, sin = rope_tables(S)
    h = x
    k_sh = None
    v_sh = None
    for layer in range(DEPTH):
        y = rmsnorm(h, norm_mix[layer])
        if layer < N_A_LAYERS:
            a = layer
            h = h + conformer_conv(y, conv_w_in[a], conv_b_in[a], conv_w_dw[a], conv_b_dw[a],
                                   conv_ln_g[a], conv_ln_b[a], conv_w_out[a], conv_b_out[a])
        else:
            j = layer - N_A_LAYERS
            h = h + dilated_attn_layer(y, k_sh, v_sh, attn_w_q[j], attn_w_o[j], cos, sin)
        h = h + sq_relu_mlp(rmsnorm(h, norm_mlp[layer]), mlp_w_in[layer], mlp_w_out[layer])
        if layer == N_A_LAYERS - 1:
            k_sh, v_sh = shared_kv(h, kv_norm, w_kv, cos, sin)
    return rmsnorm(h, final_norm)
```

```python
import math
from contextlib import ExitStack

import numpy as np
import concourse.bass as bass
import concourse.mybir as mybir
from concourse.bass_utils import run_bass_kernel_spmd

F32 = mybir.dt.float32
BF16 = mybir.dt.bfloat16
AF = mybir.ActivationFunctionType
ALU = mybir.AluOpType

D = 2048
NCH = 16
S = 2048
T = 1024
NPASS = 2
TT = 512
DFF = 8192
FG = 1024
NFG = DFF // FG
CW = 31
HALO = CW - 1
UW = HALO + T
NHEAD = 16
NKV = 4
EPS_RMS = 1e-6
EPS_LN = 1e-5
ROPE_THETA = 500000.0
SCALE = 1.0 / math.sqrt(128.0)

V_NMIX0, V_NMIX1, V_NMLP0, V_NMLP1 = 0, 16, 32, 48
V_BIN = 64
V_BDW, V_LNG, V_LNB, V_BOUT, V_KVN, V_FIN = 96, 112, 128, 144, 160, 176
V_WDW = 192
NV = V_WDW + 16 * CW

ENGS = ("pe", "act", "dve", "pool", "sp")


class Prog:
    def __init__(self, nc, es):
        self.nc = nc
        self.es = es
        self.streams = {e: [] for e in ENGS}
        self.cnt = {e: 0 for e in ENGS}
        self.sems = {}
        self.seen = {e: {} for e in ENGS}
        self.res = {}
        self.snap = {}
        self.dcnt = {}
        for e in ENGS:
            self.sem("E" + e)

    def sem(self, name):
        if name not in self.sems:
            self.sems[name] = self.es.enter_context(self.nc.semaphore(name))
        return self.sems[name]

    def _state(self, k):
        st = self.res.get(k)
        if st is None:
            st = [{}, {}]
            self.res[k] = st
        return st

    def _waits(self, eng, R, W):
        my = "E" + eng
        need = {}

        def req(s, v):
            if need.get(s, 0) < v:
                need[s] = v

        for k in R:
            st = self.res.get(k)
            if st:
                for s, v in st[0].items():
                    if s == my and eng == "pe":
                        continue
                    req(s, v)
        for k in W:
            st = self.res.get(k)
            if st:
                for s, v in st[0].items():
                    if s != my:
                        req(s, v)
                for s, v in st[1].items():
                    if s != my:
                        req(s, v)
        seen = self.seen[eng]
        waits = []
        for s, v in need.items():
            if seen.get(s, 0) < v:
                waits.append((s, v))
                seen[s] = v
                sn = self.snap.get((s, v))
                if sn:
                    for s2, v2 in sn.items():
                        if seen.get(s2, 0) < v2:
                            seen[s2] = v2
        return waits

    def _commit(self, tok, R, W):
        s, v = tok
        for k in R:
            st = self._state(k)
            if st[1].get(s, 0) < v:
                st[1][s] = v
        for k in W:
            self.res[k] = [{s: v}, {}]

    def op(self, eng, fn, R=(), W=()):
        waits = self._waits(eng, R, W)
        self.cnt[eng] += 1
        tok = ("E" + eng, self.cnt[eng])
        self.snap[tok] = dict(self.seen[eng])
        self.streams[eng].append((waits, fn, tok[0], 1))
        self._commit(tok, R, W)
        return tok

    def dma(self, q, out, in_, key, R=(), W=()):
        waits = self._waits(q, R, W)
        name = "D" + key
        self.sem(name)
        self.dcnt[name] = self.dcnt.get(name, 0) + 16
        tok = (name, self.dcnt[name])
        self.snap[tok] = dict(self.seen[q])

        def fn(e, out=out, in_=in_):
            return e.dma_start(out=out, in_=in_)

        self.streams[q].append((waits, fn, name, 16))
        self._commit(tok, R, W)
        return tok

    def alias(self, new_keys, old_keys):
        w, r = {}, {}
        for k in old_keys:
            st = self.res.get(k)
            if st:
                for s, v in st[0].items():
                    if w.get(s, 0) < v:
                        w[s] = v
                for s, v in st[1].items():
                    if r.get(s, 0) < v:
                        r[s] = v
        for k in new_keys:
            self.res[k] = [dict(w), dict(r)]

    def wait_all(self, eng, keys):
        waits = self._waits(eng, list(keys), list(keys))
        self.streams[eng].append((waits, None, None, 0))

    def emit(self, eng, e):
        sems = self.sems
        for waits, fn, sname, inc in self.streams[eng]:
            for s, v in waits:
                e.wait_ge(sems[s], v)
            if fn is not None:
                ins = fn(e)
                ins.then_inc(sems[sname], inc)


class Rot:
    def __init__(self, items):
        self.items = list(items)
        self.i = 0

    def next(self):
        x = self.items[self.i % len(self.items)]
        self.i += 1
        return x


def build_nc(debug_stage=None):
    nc = bass.Bass("TRN2", target_bir_lowering=False)

    def din(name, shape, dt=F32):
        return nc.dram_tensor(name, shape, dt, kind="ExternalInput").ap()

    x_d = din("x", [S, D])
    vecs_d = din("vecs", [128, NV])
    rope_d = din("rope", [32, 2, S])
    cst_d = din("cst", [128, 128 + 3 * 128 + 32])
    w_cin = din("conv_w_in", [D, 2 * D])
    w_cout = din("conv_w_out", [D, D])
    w_kv = din("w_kv", [D, 1024])
    w_q = din("attn_w_q", [D, D])
    w_o = din("attn_w_o", [D, D])
    w_m_in = [din("mlp_w_in0", [D, DFF]), din("mlp_w_in1", [D, DFF])]
    w_m_out = [din("mlp_w_out0", [DFF, D]), din("mlp_w_out1", [DFF, D])]
    out_d = nc.dram_tensor("out", [S, D], F32, kind="ExternalOutput").ap()
    dbg_d = None
    if debug_stage is not None:
        dbg_d = nc.dram_tensor("dbg", [128, NCH * T], F32, kind="ExternalOutput").ap()

    with ExitStack() as es:
        P = Prog(nc, es)

        def sb(name, shape, dt):
            return es.enter_context(nc.sbuf_tensor(name, shape, dt))

        hT = sb("hT", [128, NCH, T], F32)
        ybuf = sb("ybuf", [128, 8192], F32)
        kvbuf = sb("kvbuf", [128, 8192], F32)
        NWS = 4
        wbufs = [sb(f"wbuf{i}", [128, 2048], BF16) for i in range(NWS)]
        vecs = sb("vecs_sb", [128, NV], F32)
        ident = sb("ident_sb", [128, 128], F32)
        identb = sb("identb", [128, 128], BF16)
        onesb = sb("onesb", [128, 128], BF16)
        masks = sb("masks", [128, 3, 128], BF16)
        pswap = sb("pswap", [32, 32], F32)
        wdwb = sb("wdwb", [128, 16, CW], BF16)
        halo = sb("halo", [128, NCH, HALO], BF16)
        rs = [sb(f"rs{i}", [128, TT], F32) for i in range(2)]
        sq = [sb(f"sq{i}", [128, TT], BF16) for i in range(3)]
        XBYTES = nc.sbuf_bytes_remaining - 256
        XN = (XBYTES // 4) // 64 * 64
        xbuf = sb("xbuf", [128, XN], F32)
        assert XN * 4 >= 50 * 1024, XN

        yT = ybuf[:].bitcast(BF16).rearrange("p (c t) -> p c t", c=NCH)
        oacc = ybuf[:, 0:4096].rearrange("p (g t) -> p g t", g=4)
        lacc = ybuf[:, 4096:8192].rearrange("p (g t) -> p g t", g=4)
        kvb = kvbuf[:].bitcast(BF16)

        def KT(p):
            return kvb[:, p * 8192:p * 8192 + 4096].rearrange("p (h t) -> p h t", h=NKV)

        def VT(p):
            return kvb[:, p * 8192 + 4096:p * 8192 + 8192].rearrange("p (h t) -> p h t", h=NKV)

        vtile = kvbuf[:, 4096:8192].rearrange("p (c t) -> p c t", c=NCH)

        xoff = [0]

        def xcarve(nfloats):
            a = xoff[0]
            xoff[0] += nfloats
            assert xoff[0] <= XN, (xoff[0], XN)
            return xbuf[:, a:a + nfloats]

        def xreset():
            xoff[0] = 0

        xreset()
        uT = xcarve(NCH * UW // 2).bitcast(BF16).rearrange("p (c t) -> p c t", c=NCH)
        dgA = xcarve(16 * 128 // 2).bitcast(BF16).rearrange("p (k m) -> p k m", k=16)
        dgB = xcarve(15 * 128 // 2).bitcast(BF16).rearrange("p (k m) -> p k m", k=15)
        sgt = [xcarve(TT) for _ in range(2)]
        mu_t = xcarve(256)
        rstd_t = xcarve(256)
        tmp_t = xcarve(256)
        vb_t = [xcarve(128).bitcast(BF16) for _ in range(3)]
        xstage = [xcarve(2048) for _ in range(0)]
        conv_end = xoff[0]
        xs_off = conv_end
        xreset()
        hid = [xcarve(8 * T // 2).bitcast(BF16).rearrange("p (c t) -> p c t", c=8) for _ in range(2)]
        stage = [xcarve(2048) for _ in range(2)]
        mlp_end = xoff[0]
        xreset()
        QT = xcarve(NHEAD * T // 2).bitcast(BF16).rearrange("p (h t) -> p h t", h=NHEAD)
        ropet = xcarve(2 * T)
        ropev = ropet.rearrange("p (a t) -> p a t", a=2)
        Pt = [xcarve(256).bitcast(BF16) for _ in range(4)]
        Vt = [xcarve(64).bitcast(BF16) for _ in range(4)]
        q32 = [xcarve(TT) for _ in range(1)]
        t2b = [xcarve(TT) for _ in range(1)]
        att_end = xoff[0]
        assert max(conv_end, mlp_end, att_end) <= XN

        XKEYS_CONV = [("u", c) for c in range(NCH)] + ["dgA", "dgB"] + [("sg", i) for i in range(2)] + \
                     ["mu", "rstd", "tmp"] + [("vb", i) for i in range(3)]
        XKEYS_MLP = [("hid", i, j, tt) for i in range(2) for j in range(8) for tt in range(2)] + [("stage", i) for i in range(2)]
        XKEYS_ATT = [("Q", h) for h in range(NHEAD)] + ["ropet"] + [("Pt", i) for i in range(4)] + [("Vt", i) for i in range(4)] + \
                    [("q32", i) for i in range(1)] + [("t2", i) for i in range(1)]
        YKEYS = [("y", c, tt) for c in range(NCH) for tt in range(2)]

        psb = [es.enter_context(nc.psum_tensor(f"ps{i}", [128, TT], F32)) for i in range(7)]
        psbf = es.enter_context(nc.psum_tensor("psbf", [128, 1024], BF16))
        mmrot = Rot([0, 1, 2, 3])
        auxrot = Rot([4, 5, 6])
        bfrot = Rot([0, 1, 2, 3])

        def ts(tt):
            return slice(tt * TT, (tt + 1) * TT)

        def vcol(c):
            return vecs[:, c:c + 1]

        P.dma("sp", vecs[:], vecs_d, "c0", W=["vecs"])
        P.dma("sp", ident[:], cst_d[:, 0:128], "c1", W=["ident"])
        P.dma("sp", pswap[:], cst_d[0:32, 512:544], "c2", W=["pswap"])
        P.dma("pool", identb[:], cst_d[:, 0:128], "c3", W=["identb"])
        P.dma("pool", masks[:], cst_d[:, 128:512].rearrange("p (a b) -> p a b", a=3), "c4", W=["masks"])
        P.op("dve", lambda e: e.memset(onesb[:], 1.0), W=["ones"])
        P.op("dve", lambda e: e.tensor_copy(out=wdwb[:], in_=vecs[:, V_WDW:V_WDW + 16 * CW].rearrange("p (c k) -> p c k", c=16)),
             R=["vecs"], W=["wdwb"])
        P.op("dve", lambda e: e.memset(halo[:], 0.0), W=["halo"])

        wstate = {"i": 0}

        def wload(Wd, r0, nk, c0, ncols):
            s = wstate["i"] % NWS
            wstate["i"] += 1
            view = wbufs[s][:, 0:nk * ncols].rearrange("p (k c) -> p k c", k=nk)
            src = Wd[r0:r0 + nk * 128, c0:c0 + ncols].rearrange("(k p) c -> p k c", p=128)
            P.dma("pool", view, src, f"w{s}", W=[("w", s)])
            return s, view

        def mm_group(out_ap, pairs, R, W):
            def fn(e, out_ap=out_ap, pairs=pairs):
                n = len(pairs)
                ins = None
                for i, (l, r) in enumerate(pairs):
                    ins = e.matmul(out_ap, l, r, start=(i == 0), stop=(i == n - 1))
                return ins
            return P.op("pe", fn, R=R, W=W)

        def rmsnorm(gcol, out_f32_inplace=False):
            for tt in range(2):
                bank = auxrot.next()
                for c in range(NCH):
                    sqi = P_sq.next()
                    P.op("act", lambda e, c=c, sqi=sqi, tt=tt: e.activation(out=sq[sqi][:], in_=hT[:, c, ts(tt)], func=AF.Square),
                         R=[("h", c, tt)], W=[("sq", sqi)])
                    P.op("pe", lambda e, c=c, sqi=sqi, bank=bank: e.matmul(psb[bank][:], onesb[:], sq[sqi][:], start=(c == 0), stop=(c == NCH - 1)),
                         R=[("sq", sqi), "ones"], W=[("ps", bank)])
                ri = P_rs.next()
                P.op("act", lambda e, ri=ri, bank=bank: e.activation(out=rs[ri][:], in_=psb[bank][:], func=AF.Sqrt, scale=1.0 / D, bias=EPS_RMS),
                     R=[("ps", bank)], W=[("rs", ri)])
                P.op("dve", lambda e, ri=ri: e.reciprocal(out=rs[ri][:], in_=rs[ri][:]), R=[("rs", ri)], W=[("rs", ri)])
                for c in range(NCH):
                    if out_f32_inplace:
                        P.op("dve", lambda e, c=c, tt=tt, ri=ri: e.scalar_tensor_tensor(
                            out=hT[:, c, ts(tt)], in0=hT[:, c, ts(tt)], scalar=vcol(gcol + c), in1=rs[ri][:], op0=ALU.mult, op1=ALU.mult),
                            R=[("h", c, tt), ("rs", ri), "vecs"], W=[("h", c, tt)])
                    else:
                        P.op("dve", lambda e, c=c, tt=tt, ri=ri: e.scalar_tensor_tensor(
                            out=yT[:, c, ts(tt)], in0=hT[:, c, ts(tt)], scalar=vcol(gcol + c), in1=rs[ri][:], op0=ALU.mult, op1=ALU.mult),
                            R=[("h", c, tt), ("rs", ri), "vecs"], W=[("y", c, tt)])

        P_sq = Rot([0, 1, 2])
        P_rs = Rot([0, 1])

        def dense(Wd, r0, nk, col_tiles, ncols, rhs_fn, rkeys_fn, evac_fn):
            for c0 in col_tiles:
                s, view = wload(Wd, r0, nk, c0, ncols)
                for mi in range(ncols // 128):
                    for tt in range(2):
                        bank = mmrot.next()
                        pairs = [(view[:, k, mi * 128:(mi + 1) * 128], rhs_fn(k, tt)) for k in range(nk)]
                        R = [("w", s)]
                        for k in range(nk):
                            R.extend(rkeys_fn(k, tt))
                        mm_group(psb[bank][:], pairs, R, [("ps", bank)])
                        evac_fn((c0 // 128) + mi, tt, bank)

        def evac_add_h(bias_col=None):
            def f(m, tt, bank):
                if bias_col is None:
                    P.op("dve", lambda e: e.tensor_tensor(out=hT[:, m, ts(tt)], in0=psb[bank][:], in1=hT[:, m, ts(tt)], op=ALU.add),
                         R=[("ps", bank), ("h", m, tt)], W=[("h", m, tt)])
                else:
                    P.op("dve", lambda e: e.scalar_tensor_tensor(out=hT[:, m, ts(tt)], in0=psb[bank][:], scalar=vcol(bias_col + m),
                                                                 in1=hT[:, m, ts(tt)], op0=ALU.add, op1=ALU.add),
                         R=[("ps", bank), ("h", m, tt), "vecs"], W=[("h", m, tt)])
            return f

        def y_rhs(k, tt):
            return yT[:, k, ts(tt)]

        def y_keys(k, tt):
            return [("y", k, tt)]

        def mlp(layer):
            rmsnorm(V_NMLP0 if layer == 0 else V_NMLP1)
            Win, Wout = w_m_in[layer], w_m_out[layer]

            def w_in_group(G):
                hb = G % 2

                def evac(m, tt, bank):
                    j = m - G * 8
                    sqi = P_sq.next()
                    P.op("act", lambda e: e.activation(out=sq[sqi][:], in_=psb[bank][:], func=AF.Relu),
                         R=[("ps", bank)], W=[("sq", sqi)])
                    P.op("dve", lambda e: e.tensor_tensor(out=hid[hb][:, j, ts(tt)], in0=sq[sqi][:], in1=sq[sqi][:], op=ALU.mult),
                         R=[("sq", sqi)], W=[("hid", hb, j, tt)])
                dense(Win, 0, NCH, [(G * 8 + j) * 128 for j in range(8)], 128, y_rhs, y_keys, evac)

            def w_out_group(G):
                hb = G % 2
                dense(Wout, G * FG, 8, [c * 256 for c in range(8)], 256,
                      lambda k, tt: hid[hb][:, k, ts(tt)], lambda k, tt: [("hid", hb, k, tt)], evac_add_h())

            w_in_group(0)
            for G in range(NFG):
                if G + 1 < NFG:
                    w_in_group(G + 1)
                w_out_group(G)

        def rope_evac(bank, dst, tok0, dst_keys):
            qi = P_q32.next()
            P.op("act", lambda e: e.activation(out=dst, in_=psb[bank][:], func=AF.Copy),
                 R=[("ps", bank)], W=dst_keys)
            P.op("act", lambda e: e.activation(out=q32[qi][0:32, :], in_=psb[bank][0:32, :], func=AF.Copy),
                 R=[("ps", bank)], W=[("q32", qi)])
            b2 = auxrot.next()
            P.op("pe", lambda e: e.matmul(psb[b2][0:32, :], pswap[:], q32[qi][0:32, :], start=True, stop=True),
                 R=[("q32", qi), "pswap"], W=[("ps", b2)])
            P.op("dve", lambda e: e.tensor_tensor(out=q32[qi][0:32, :], in0=q32[qi][0:32, :], in1=ropev[0:32, 0, tok0:tok0 + TT], op=ALU.mult),
                 R=[("q32", qi), "ropet"], W=[("q32", qi)])
            P.op("dve", lambda e: e.tensor_tensor(out=t2b[qi][0:32, :], in0=psb[b2][0:32, :], in1=ropev[0:32, 1, tok0:tok0 + TT], op=ALU.mult),
                 R=[("ps", b2), "ropet"], W=[("t2", qi)])
            P.op("dve", lambda e: e.tensor_tensor(out=dst[0:32, :], in0=q32[qi][0:32, :], in1=t2b[qi][0:32, :], op=ALU.add),
                 R=[("q32", qi), ("t2", qi)], W=dst_keys)

        P_q32 = Rot([0])

        def dump_h(p):
            P.dma("sp", dbg_d, hT[:].rearrange("p c t -> p (c t)"), "dbg",
                  R=[("h", c, tt) for c in range(NCH) for tt in range(2)], W=["dbgout"])
            P.wait_all("sp", ["dbgout"])

        def dump_y(p):
            for c in range(NCH):
                for tt in range(2):
                    P.op("act", lambda e, c=c, tt=tt: e.activation(out=hT[:, c, ts(tt)], in_=yT[:, c, ts(tt)], func=AF.Copy),
                         R=[("y", c, tt)], W=[("h", c, tt)])
            dump_h(p)

        for p in range(NPASS):
            tokp = p * T
            P.alias([("stage", i) for i in range(2)], XKEYS_CONV + XKEYS_ATT + XKEYS_MLP)
            for b in range(T // 128):
                si = b % 2
                P.dma("sp", stage[si], x_d[tokp + b * 128:tokp + (b + 1) * 128, :], f"xs{si}", W=[("stage", si)])
                for cg in range(4):
                    bank = auxrot.next()

                    def fn(e, si=si, cg=cg, bank=bank):
                        ins = None
                        for j in range(4):
                            c = cg * 4 + j
                            ins = e.transpose(psb[bank][:, j * 128:(j + 1) * 128], stage[si][:, c * 128:(c + 1) * 128], ident[:])
                        return ins
                    P.op("pe", fn, R=[("stage", si), "ident"], W=[("ps", bank)])
                    tt = (b * 128) // TT
                    P.op("act", lambda e, cg=cg, bank=bank, b=b: e.activation(
                        out=hT[:, cg * 4:cg * 4 + 4, b * 128:(b + 1) * 128],
                        in_=psb[bank][:].rearrange("p (j t) -> p j t", j=4), func=AF.Copy),
                        R=[("ps", bank)], W=[("h", cg * 4 + j, tt) for j in range(4)])
            if debug_stage == "load" and p == 0:
                dump_h(p)
                break

            rmsnorm(V_NMIX0)
            if debug_stage == "norm0" and p == 0:
                dump_y(p)
                break
            P.alias(XKEYS_CONV, XKEYS_CONV + XKEYS_ATT + XKEYS_MLP)
            P.op("dve", lambda e: e.tensor_copy(out=uT[:, :, 0:HALO], in_=halo[:]), R=["halo"], W=[("u", c) for c in range(NCH)])
            for j in range(NCH):
                sg_, gview = wload(w_cin, 0, NCH, D + j * 128, 128)
                sa_, aview = wload(w_cin, 0, NCH, j * 128, 128)
                for tt in range(2):
                    bank = mmrot.next()
                    mm_group(psb[bank][:], [(gview[:, k, :], yT[:, k, ts(tt)]) for k in range(NCH)],
                             [("w", sg_)] + [("y", k, tt) for k in range(NCH)], [("ps", bank)])
                    si = P_sg.next() if False else (j * 2 + tt) % 2
                    P.op("act", lambda e, bank=bank, si=si, j=j: e.activation(out=sgt[si], in_=psb[bank][:], func=AF.Sigmoid,
                                                                               bias=vcol(V_BIN + 16 + j)),
                         R=[("ps", bank), "vecs"], W=[("sg", si)])
                    bank2 = mmrot.next()
                    mm_group(psb[bank2][:], [(aview[:, k, :], yT[:, k, ts(tt)]) for k in range(NCH)],
                             [("w", sa_)] + [("y", k, tt) for k in range(NCH)], [("ps", bank2)])
                    P.op("dve", lambda e, bank2=bank2, si=si, j=j, tt=tt: e.scalar_tensor_tensor(
                        out=uT[:, j, HALO + tt * TT:HALO + (tt + 1) * TT], in0=psb[bank2][:], scalar=vcol(V_BIN + j),
                        in1=sgt[si], op0=ALU.add, op1=ALU.mult),
                        R=[("ps", bank2), ("sg", si), "vecs"], W=[("u", j)])
            P.op("dve", lambda e: e.tensor_copy(out=halo[:], in_=uT[:, :, T:T + HALO]), R=[("u", c) for c in range(NCH)], W=["halo"])
            if debug_stage == "glu" and p == 0:
                for c in range(NCH):
                    for tt in range(2):
                        P.op("act", lambda e, c=c, tt=tt: e.activation(out=hT[:, c, ts(tt)], in_=uT[:, c, HALO + tt * TT:HALO + (tt + 1) * TT], func=AF.Copy),
                             R=[("u", c)], W=[("h", c, tt)])
                dump_h(p)
                break

            VKEYS = [("v", c) for c in range(NCH)]
            P.alias(VKEYS, [("KT", h, 1) for h in range(NKV)] + [("VT", h, 1) for h in range(NKV)] + VKEYS)
            for q in range(T // 256):
                tq = q * 256
                b1 = auxrot.next()
                b2 = auxrot.next()
                for c in range(NCH):
                    P.op("dve", lambda e, c=c: e.tensor_tensor(
                        out=dgA, in0=identb[:].unsqueeze(1).broadcast_to([128, 16, 128]),
                        in1=wdwb[:, c, 0:16].unsqueeze(2).broadcast_to([128, 16, 128]), op=ALU.mult),
                        R=["identb", "wdwb"], W=["dgA"])
                    P.op("dve", lambda e, c=c: e.tensor_tensor(
                        out=dgB, in0=identb[:].unsqueeze(1).broadcast_to([128, 15, 128]),
                        in1=wdwb[:, c, 16:CW].unsqueeze(2).broadcast_to([128, 15, 128]), op=ALU.mult),
                        R=["identb", "wdwb"], W=["dgB"])
                    bank = mmrot.next()

                    def fnA(e, c=c, bank=bank, tq=tq):
                        ins = None
                        for k in range(16):
                            ins = e.matmul(psb[bank][:, 0:256], dgA[:, k, :], uT[:, c, tq + k:tq + k + 256], start=(k == 0), stop=False)
                        return ins

                    def fnB(e, c=c, bank=bank, tq=tq):
                        ins = None
                        for k in range(16, CW):
                            ins = e.matmul(psb[bank][:, 0:256], dgB[:, k - 16, :], uT[:, c, tq + k:tq + k + 256], start=False, stop=(k == CW - 1))
                        return ins
                    P.op("pe", fnA, R=["dgA", ("u", c)], W=[("ps", bank)])
                    P.op("pe", fnB, R=["dgB", ("u", c)], W=[("ps", bank)])
                    P.op("act", lambda e, c=c, bank=bank: e.activation(out=vtile[:, c, :], in_=psb[bank][:, 0:256], func=AF.Identity,
                                                                        bias=vcol(V_BDW + c)),
                         R=[("ps", bank), "vecs"], W=[("v", c)])
                    sqi = P_sq.next()
                    P.op("act", lambda e, c=c, bank=bank, sqi=sqi: e.activation(out=sq[sqi][:, 0:256], in_=psb[bank][:, 0:256], func=AF.Square,
                                                                                 bias=vcol(V_BDW + c)),
                         R=[("ps", bank), "vecs"], W=[("sq", sqi)])
                    vi = P_vb.next()
                    P.op("dve", lambda e, c=c, vi=vi: e.tensor_copy(out=vb_t[vi], in_=vtile[:, c, :]), R=[("v", c)], W=[("vb", vi)])
                    P.op("pe", lambda e, c=c, vi=vi, b1=b1: e.matmul(psb[b1][:, 0:256], onesb[:], vb_t[vi], start=(c == 0), stop=(c == NCH - 1)),
                         R=[("vb", vi), "ones"], W=[("ps", b1)])
                    P.op("pe", lambda e, c=c, sqi=sqi, b2=b2: e.matmul(psb[b2][:, 0:256], onesb[:], sq[sqi][:, 0:256], start=(c == 0), stop=(c == NCH - 1)),
                         R=[("sq", sqi), "ones"], W=[("ps", b2)])
                P.op("act", lambda e, b1=b1: e.activation(out=mu_t, in_=psb[b1][:, 0:256], func=AF.Copy, scale=1.0 / D),
                     R=[("ps", b1)], W=["mu"])
                P.op("dve", lambda e: e.tensor_tensor(out=tmp_t, in0=mu_t, in1=mu_t, op=ALU.mult), R=["mu"], W=["tmp"])
                P.op("dve", lambda e, b2=b2: e.scalar_tensor_tensor(out=rstd_t, in0=psb[b2][:, 0:256], scalar=1.0 / D, in1=tmp_t,
                                                                    op0=ALU.mult, op1=ALU.subtract),
                     R=[("ps", b2), "tmp"], W=["rstd"])
                P.op("act", lambda e: e.activation(out=rstd_t, in_=rstd_t, func=AF.Sqrt, bias=EPS_LN), R=["rstd"], W=["rstd"])
                P.op("dve", lambda e: e.reciprocal(out=rstd_t, in_=rstd_t), R=["rstd"], W=["rstd"])
                P.op("dve", lambda e: e.tensor_tensor(out=vtile, in0=vtile, in1=mu_t.unsqueeze(1).broadcast_to([128, NCH, 256]), op=ALU.subtract),
                     R=VKEYS + ["mu"], W=VKEYS)
                P.op("dve", lambda e: e.tensor_tensor(out=vtile, in0=vtile, in1=rstd_t.unsqueeze(1).broadcast_to([128, NCH, 256]), op=ALU.mult),
                     R=VKEYS + ["rstd"], W=VKEYS)
                for c in range(NCH):
                    P.op("act", lambda e, c=c, tq=tq: e.activation(out=yT[:, c, tq:tq + 256], in_=vtile[:, c, :], func=AF.Silu,
                                                                    scale=vcol(V_LNG + c), bias=vcol(V_LNB + c)),
                         R=[("v", c), "vecs"], W=[("y", c, tq // TT)])
            if debug_stage == "conv" and p == 0:
                dump_y(p)
                break
            dense(w_cout, 0, NCH, [m * 128 for m in range(NCH)], 128, y_rhs, y_keys, evac_add_h(V_BOUT))
            if debug_stage == "mix0" and p == 0:
                dump_h(p)
                break

            P.alias(XKEYS_MLP, XKEYS_CONV + XKEYS_ATT + XKEYS_MLP)
            mlp(0)
            if debug_stage == "mlp0" and p == 0:
                dump_h(p)
                break

            P.alias(XKEYS_ATT, XKEYS_CONV + XKEYS_ATT + XKEYS_MLP)
            P.alias([("KT", h, 1) for h in range(NKV)] + [("VT", h, 1) for h in range(NKV)],
                    [("KT", h, 1) for h in range(NKV)] + [("VT", h, 1) for h in range(NKV)] + VKEYS)
            P.dma("sp", ropev[0:32, :, :], rope_d[:, :, tokp:tokp + T], "rope", W=["ropet"])
            rmsnorm(V_KVN)
            for h in range(NKV):
                dense(w_kv, 0, NCH, [h * 128], 128, y_rhs, y_keys,
                      lambda m, tt, bank, h=h: rope_evac(bank, KT(p)[:, h, ts(tt)], tt * TT, [("KT", h, p)]))
            for h in range(NKV):
                def evac_v(m, tt, bank, h=h, p=p):
                    dstv = VT(p)[:, h, ts(tt)]
                    P.op("act", lambda e: e.activation(out=dstv, in_=psb[bank][:], func=AF.Copy),
                         R=[("ps", bank)], W=[("VT", h, p)])
                dense(w_kv, 0, NCH, [512 + h * 128], 128, y_rhs, y_keys, evac_v)

            rmsnorm(V_NMIX1)
            for hd in range(NHEAD):
                dense(w_q, 0, NCH, [hd * 128], 128, y_rhs, y_keys,
                      lambda m, tt, bank, hd=hd: rope_evac(bank, QT[:, hd, ts(tt)], tt * TT, [("Q", hd)]))
            if debug_stage == "q" and p == 0:
                for c in range(NCH):
                    for tt in range(2):
                        P.op("act", lambda e, c=c, tt=tt: e.activation(out=hT[:, c, ts(tt)], in_=QT[:, c, ts(tt)], func=AF.Copy),
                             R=[("Q", c)], W=[("h", c, tt)])
                dump_h(p)
                break

            if debug_stage == "kvcmp" and p == 1:
                P.dma("sp", dbg_d[:, 8192:16384], kvbuf[:], "dbg",
                      R=[("KT", h, pp) for h in range(NKV) for pp in range(2)] + [("VT", h, pp) for h in range(NKV) for pp in range(2)], W=["dbgout"])
                P.wait_all("sp", ["dbgout"])
                break
            P.alias(["oacc", "lacc"], YKEYS + ["oacc", "lacc"])
            vt_cache = {}

            def kt_ap(h, g0, r, n):
                pp = g0 // T
                l0 = g0 - pp * T
                assert l0 + r * (n - 1) < T
                return KT(pp)[:, h, l0:l0 + r * (n - 1) + 1:r], VT(pp)[:, h, l0:l0 + r * (n - 1) + 1:r], pp

            for h in range(NKV):
                work = []
                for n in range(T // 128):
                    N = p * 8 + n
                    kts = []
                    if N >= 1:
                        kts.append(((N - 1) * 128, 128, 1))
                    kts.append((N * 128, 128, 0))
                    work.append((1, n * 128, 128, kts))
                for rho in range(4):
                    for nn in range(2):
                        N4 = 2 * p + nn
                        kts = []
                        if N4 >= 1:
                            kts.append((512 * (N4 - 1) + rho, 128, 1))
                        kts.append((512 * N4 + rho, 128, 0))
                        work.append((4, 512 * N4 + rho - tokp, 128, kts))
                for rho in range(16):
                    if p == 0:
                        work.append((16, rho, 64, [(rho, 64, 0)]))
                    else:
                        work.append((16, rho, 64, [(rho, 64, 3), (T + rho, 64, 0)]))
                first_branch_done = set()
                for (r, qs, qn, kts) in work:
                    ncol = 4 * qn
                    qap = QT[:, 4 * h:4 * h + 4, qs:qs + r * (qn - 1) + 1:r]
                    bo = auxrot.next()
                    bl = auxrot.next()
                    for ki, (g0, kn, mid) in enumerate(kts):
                        kap, vap, pp = kt_ap(h, g0, r, kn)
                        bs = mmrot.next()
                        P.op("pe", lambda e, bs=bs, kap=kap, qap=qap, kn=kn, ncol=ncol, qn=qn: e.matmul(
                            psb[bs][0:kn, 0:ncol].rearrange("p (g q) -> p g q", g=4), kap, qap, start=True, stop=True),
                            R=[("KT", h, pp)] + [("Q", 4 * h + g) for g in range(4)], W=[("ps", bs)])
                        pi = P_pt.next()
                        P.op("act", lambda e, bs=bs, pi=pi, kn=kn, ncol=ncol: e.activation(out=Pt[pi][0:kn, 0:ncol], in_=psb[bs][0:kn, 0:ncol],
                                                                                          func=AF.Exp, scale=SCALE),
                             R=[("ps", bs)], W=[("Pt", pi)])
                        if mid != 3:
                            P.op("dve", lambda e, pi=pi, kn=kn, ncol=ncol, qn=qn, mid=mid: e.tensor_tensor(
                                out=Pt[pi][0:kn, 0:ncol].rearrange("p (g q) -> p g q", g=4),
                                in0=Pt[pi][0:kn, 0:ncol].rearrange("p (g q) -> p g q", g=4),
                                in1=masks[0:kn, mid, 0:qn].unsqueeze(1).broadcast_to([kn, 4, qn]), op=ALU.mult),
                                R=[("Pt", pi), "masks"], W=[("Pt", pi)])
                        ck = (h, g0, r, kn)
                        if ck in vt_cache and vt_cache[ck][1] > P_vt.i - 3:
                            vi = vt_cache[ck][0]
                        else:
                            vi = P_vt.next()
                            vt_cache[ck] = (vi, P_vt.i)
                            qb = bfrot.next()
                            P.op("pe", lambda e, qb=qb, vap=vap, kn=kn: e.transpose(psbf[0:kn, qb * 128:(qb + 1) * 128], vap, identb[:]),
                                 R=[("VT", h, pp), "identb"], W=[("psbf", qb)])
                            P.op("act", lambda e, qb=qb, vi=vi, kn=kn: e.activation(out=Vt[vi][0:kn, :], in_=psbf[0:kn, qb * 128:(qb + 1) * 128], func=AF.Copy),
                                 R=[("psbf", qb)], W=[("Vt", vi)])
                        last = (ki == len(kts) - 1)
                        P.op("pe", lambda e, bo=bo, vi=vi, pi=pi, kn=kn, ncol=ncol, ki=ki, last=last: e.matmul(
                            psb[bo][:, 0:ncol], Vt[vi][0:kn, :], Pt[pi][0:kn, 0:ncol], start=(ki == 0), stop=last),
                            R=[("Vt", vi), ("Pt", pi)], W=[("ps", bo)])
                        P.op("pe", lambda e, bl=bl, pi=pi, kn=kn, ncol=ncol, ki=ki, last=last: e.matmul(
                            psb[bl][:, 0:ncol], onesb[0:kn, :], Pt[pi][0:kn, 0:ncol], start=(ki == 0), stop=last),
                            R=[("Pt", pi), "ones"], W=[("ps", bl)])
                    oap = oacc[:, :, qs:qs + r * (qn - 1) + 1:r]
                    lap = lacc[:, :, qs:qs + r * (qn - 1) + 1:r]
                    ops_ = psb[bo][:, 0:ncol].rearrange("p (g q) -> p g q", g=4)
                    lps_ = psb[bl][:, 0:ncol].rearrange("p (g q) -> p g q", g=4)
                    if r == 1:
                        P.op("act", lambda e, oap=oap, ops_=ops_: e.activation(out=oap, in_=ops_, func=AF.Copy),
                             R=[("ps", bo)], W=["oacc"])
                        P.op("act", lambda e, lap=lap, lps_=lps_: e.activation(out=lap, in_=lps_, func=AF.Copy),
                             R=[("ps", bl)], W=["lacc"])
                    else:
                        P.op("dve", lambda e, oap=oap, ops_=ops_: e.tensor_tensor(out=oap, in0=ops_, in1=oap, op=ALU.add),
                             R=[("ps", bo), "oacc"], W=["oacc"])
                        P.op("dve", lambda e, lap=lap, lps_=lps_: e.tensor_tensor(out=lap, in0=lps_, in1=lap, op=ALU.add),
                             R=[("ps", bl), "lacc"], W=["lacc"])
                P.op("dve", lambda e: e.reciprocal(out=lacc, in_=lacc), R=["lacc"], W=["lacc"])
                for g in range(4):
                    P.op("dve", lambda e, g=g, h=h: e.tensor_tensor(out=QT[:, 4 * h + g, :], in0=oacc[:, g, :], in1=lacc[:, g, :], op=ALU.mult),
                         R=["oacc", "lacc"], W=[("Q", 4 * h + g)])
            P.alias(YKEYS, YKEYS + ["oacc", "lacc"])
            if debug_stage == "kvcmp" and p == 0:
                P.dma("sp", dbg_d[:, 0:8192], kvbuf[:], "dbg",
                      R=[("KT", h, pp) for h in range(NKV) for pp in range(1)] + [("VT", h, pp) for h in range(NKV) for pp in range(1)], W=["dbgout"])
            if debug_stage == "attn" and p == 0:
                for c in range(NCH):
                    for tt in range(2):
                        P.op("act", lambda e, c=c, tt=tt: e.activation(out=hT[:, c, ts(tt)], in_=QT[:, c, ts(tt)], func=AF.Copy),
                             R=[("Q", c)], W=[("h", c, tt)])
                dump_h(p)
                break
            dense(w_o, 0, NCH, [m * 128 for m in range(NCH)], 128,
                  lambda k, tt: QT[:, k, ts(tt)], lambda k, tt: [("Q", k)], evac_add_h())
            if debug_stage == "mix1" and p == 0:
                dump_h(p)
                break

            P.alias(XKEYS_MLP, XKEYS_CONV + XKEYS_ATT + XKEYS_MLP)
            mlp(1)
            if debug_stage == "mlp1" and p == 0:
                dump_h(p)
                break

            rmsnorm(V_FIN, out_f32_inplace=True)
            for b in range(T // 128):
                si = b % 2
                tt = (b * 128) // TT
                for cg in range(4):
                    bank = auxrot.next()

                    def fn(e, cg=cg, bank=bank, b=b):
                        ins = None
                        for j in range(4):
                            c = cg * 4 + j
                            ins = e.transpose(psb[bank][:, j * 128:(j + 1) * 128], hT[:, c, b * 128:(b + 1) * 128], ident[:])
                        return ins
                    P.op("pe", fn, R=[("h", cg * 4 + j, tt) for j in range(4)] + ["ident"], W=[("ps", bank)])
                    P.op("act", lambda e, cg=cg, bank=bank, si=si: e.activation(out=stage[si][:, cg * 512:(cg + 1) * 512], in_=psb[bank][:], func=AF.Copy),
                         R=[("ps", bank)], W=[("stage", si)])
                P.dma("sp", out_d[tokp + b * 128:tokp + (b + 1) * 128, :], stage[si], f"os{si}", R=[("stage", si)], W=[("outrow", b % 2)])
        P.wait_all("sp", [("outrow", 0), ("outrow", 1)])

        block = es.enter_context(nc.Block())

        @block.tensor
        def _(e):
            P.emit("pe", e)

        @block.scalar
        def _(e):
            P.emit("act", e)

        @block.vector
        def _(e):
            P.emit("dve", e)

        @block.gpsimd
        def _(e):
            P.emit("pool", e)

        @block.sync
        def _(e):
            P.emit("sp", e)
    return nc


P_sg = None
P_pt = Rot([0, 1, 2, 3])
P_vt = Rot([0, 1, 2, 3])
P_vb = Rot([0, 1, 2])


def _reset_rots():
    global P_pt, P_vt, P_vb
    P_pt = Rot([0, 1, 2, 3])
    P_vt = Rot([0, 1, 2, 3])
    P_vb = Rot([0, 1, 2])


def host_consts():
    half = 16
    pos = np.arange(S, dtype=np.float32)
    inv = (np.float32(ROPE_THETA) ** (-np.arange(0, 32, 2, dtype=np.float32) / np.float32(32))).astype(np.float32)
    ang = (pos[:, None] * inv[None, :]).astype(np.float32)
    cos = np.cos(ang).astype(np.float32).T
    sin = np.sin(ang).astype(np.float32).T
    rope = np.zeros((32, 2, S), np.float32)
    rope[0:16, 0] = cos
    rope[16:32, 0] = cos
    rope[0:16, 1] = -sin
    rope[16:32, 1] = sin
    cst = np.zeros((128, 128 + 3 * 128 + 32), np.float32)
    cst[:, 0:128] = np.eye(128, dtype=np.float32)
    k = np.arange(128)[:, None]
    q = np.arange(128)[None, :]
    cst[:, 128:256] = (k <= q)
    cst[:, 256:384] = (k >= q)
    cst[:, 384:512] = 1.0
    for m in range(32):
        cst[(m + 16) % 32, 512 + m] = 1.0
    return rope, cst


def pack_vecs(inp):
    v = np.zeros((128, NV), np.float32)

    def put(col, vec):
        n = vec.shape[0] // 128
        v[:, col:col + n] = vec.reshape(n, 128).T

    put(V_NMIX0, inp["norm_mix"][0])
    put(V_NMIX1, inp["norm_mix"][1])
    put(V_NMLP0, inp["norm_mlp"][0])
    put(V_NMLP1, inp["norm_mlp"][1])
    put(V_BIN, inp["conv_b_in"][0])
    put(V_BDW, inp["conv_b_dw"][0])
    put(V_LNG, inp["conv_ln_g"][0])
    put(V_LNB, inp["conv_ln_b"][0])
    put(V_BOUT, inp["conv_b_out"][0])
    put(V_KVN, inp["kv_norm"])
    put(V_FIN, inp["final_norm"])
    wdw = inp["conv_w_dw"][0]
    v[:, V_WDW:] = wdw.T.reshape(16, 128, CW).transpose(1, 0, 2).reshape(128, 16 * CW)
    return v


def make_in_map(inp, xi, rope, cst, vecs):
    f = lambda a: np.ascontiguousarray(a, dtype=np.float32)
    return {
        "x": f(xi), "vecs": vecs, "rope": rope, "cst": cst,
        "conv_w_in": f(inp["conv_w_in"][0]), "conv_w_out": f(inp["conv_w_out"][0]),
        "w_kv": f(inp["w_kv"]), "attn_w_q": f(inp["attn_w_q"][0]), "attn_w_o": f(inp["attn_w_o"][0]),
        "mlp_w_in0": f(inp["mlp_w_in"][0]), "mlp_w_in1": f(inp["mlp_w_in"][1]),
        "mlp_w_out0": f(inp["mlp_w_out"][0]), "mlp_w_out1": f(inp["mlp_w_out"][1]),
    }


def kernel(**inputs):
    inp = {k: np.asarray(v) for k, v in inputs.items()}
    _reset_rots()
    nc = build_nc()
    rope, cst = host_consts()
    vecs = pack_vecs(inp)
    n = 8
    in_maps = [make_in_map(inp, inp["x"][i], rope, cst, vecs) for i in range(n)]
    res = run_bass_kernel_spmd(nc, in_maps, core_ids=list(range(n)))
    out = np.stack([np.asarray(res.results[i]["out"], dtype=np.float32).reshape(S, D) for i in range(n)], axis=0)
    return out
```

```python
import math
from contextlib import ExitStack

import numpy as np
import concourse.bass as bass
import concourse.mybir as mybir
from concourse.bass_utils import run_bass_kernel_spmd

F32 = mybir.dt.float32
BF16 = mybir.dt.bfloat16
AF = mybir.ActivationFunctionType
ALU = mybir.AluOpType

D = 2048
NCH = 16
S = 2048
T = 1024
NPASS = 2
TT = 512
DFF = 8192
FG = 1024
NFG = DFF // FG
CW = 31
HALO = CW - 1
UW = HALO + T
NHEAD = 16
NKV = 4
EPS_RMS = 1e-6
EPS_LN = 1e-5
ROPE_THETA = 500000.0
SCALE = 1.0 / math.sqrt(128.0)

V_NMIX0, V_NMIX1, V_NMLP0, V_NMLP1 = 0, 16, 32, 48
V_BIN = 64
V_BDW, V_LNG, V_LNB, V_BOUT, V_KVN, V_FIN = 96, 112, 128, 144, 160, 176
V_WDW = 192
NV = V_WDW + 16 * CW

ENGS = ("pe", "act", "dve", "pool", "sp")


class Prog:
    def __init__(self, nc, es):
        self.nc = nc
        self.es = es
        self.streams = {e: [] for e in ENGS}
        self.cnt = {e: 0 for e in ENGS}
        self.sems = {}
        self.seen = {e: {} for e in ENGS}
        self.res = {}
        self.snap = {}
        self.dcnt = {}
        for e in ENGS:
            self.sem("E" + e)

    def sem(self, name):
        if name not in self.sems:
            self.sems[name] = self.es.enter_context(self.nc.semaphore(name))
        return self.sems[name]

    def _state(self, k):
        st = self.res.get(k)
        if st is None:
            st = [{}, {}]
            self.res[k] = st
        return st

    def _waits(self, eng, R, W):
        my = "E" + eng
        need = {}

        def req(s, v):
            if need.get(s, 0) < v:
                need[s] = v

        for k in R:
            st = self.res.get(k)
            if st:
                for s, v in st[0].items():
                    if s == my and eng == "pe":
                        continue
                    req(s, v)
        for k in W:
            st = self.res.get(k)
            if st:
                for s, v in st[0].items():
                    if s != my:
                        req(s, v)
                for s, v in st[1].items():
                    if s != my:
                        req(s, v)
        seen = self.seen[eng]
        waits = []
        for s, v in need.items():
            if seen.get(s, 0) < v:
                waits.append((s, v))
                seen[s] = v
                sn = self.snap.get((s, v))
                if sn:
                    for s2, v2 in sn.items():
                        if seen.get(s2, 0) < v2:
                            seen[s2] = v2
        return waits

    def _commit(self, tok, R, W):
        s, v = tok
        for k in R:
            st = self._state(k)
            if st[1].get(s, 0) < v:
                st[1][s] = v
        for k in W:
            self.res[k] = [{s: v}, {}]

    def op(self, eng, fn, R=(), W=()):
        waits = self._waits(eng, R, W)
        self.cnt[eng] += 1
        tok = ("E" + eng, self.cnt[eng])
        self.snap[tok] = dict(self.seen[eng])
        self.streams[eng].append((waits, fn, tok[0], 1))
        self._commit(tok, R, W)
        return tok

    def dma(self, q, out, in_, key, R=(), W=()):
        waits = self._waits(q, R, W)
        name = "D" + key
        self.sem(name)
        self.dcnt[name] = self.dcnt.get(name, 0) + 16
        tok = (name, self.dcnt[name])
        self.snap[tok] = dict(self.seen[q])

        def fn(e, out=out, in_=in_):
            return e.dma_start(out=out, in_=in_)

        self.streams[q].append((waits, fn, name, 16))
        self._commit(tok, R, W)
        return tok

    def alias(self, new_keys, old_keys):
        w, r = {}, {}
        for k in old_keys:
            st = self.res.get(k)
            if st:
                for s, v in st[0].items():
                    if w.get(s, 0) < v:
                        w[s] = v
                for s, v in st[1].items():
                    if r.get(s, 0) < v:
                        r[s] = v
        for k in new_keys:
            self.res[k] = [dict(w), dict(r)]

    def wait_all(self, eng, keys):
        waits = self._waits(eng, list(keys), list(keys))
        self.streams[eng].append((waits, None, None, 0))

    def emit(self, eng, e):
        sems = self.sems
        for waits, fn, sname, inc in self.streams[eng]:
            for s, v in waits:
                e.wait_ge(sems[s], v)
            if fn is not None:
                ins = fn(e)
                ins.then_inc(sems[sname], inc)


class Rot:
    def __init__(self, items):
        self.items = list(items)
        self.i = 0

    def next(self):
        x = self.items[self.i % len(self.items)]
        self.i += 1
        return x


def build_nc(debug_stage=None):
    nc = bass.Bass("TRN2", target_bir_lowering=False)

    def din(name, shape, dt=F32):
        return nc.dram_tensor(name, shape, dt, kind="ExternalInput").ap()

    x_d = din("x", [S, D])
    vecs_d = din("vecs", [128, NV])
    rope_d = din("rope", [32, 2, S])
    cst_d = din("cst", [128, 128 + 3 * 128 + 32])
    w_cin = din("conv_w_in", [D, 2 * D])
    w_cout = din("conv_w_out", [D, D])
    w_kv = din("w_kv", [D, 1024])
    w_q = din("attn_w_q", [D, D])
    w_o = din("attn_w_o", [D, D])
    w_m_in = [din("mlp_w_in0", [D, DFF]), din("mlp_w_in1", [D, DFF])]
    w_m_out = [din("mlp_w_out0", [DFF, D]), din("mlp_w_out1", [DFF, D])]
    out_d = nc.dram_tensor("out", [S, D], F32, kind="ExternalOutput").ap()
    dbg_d = None
    if debug_stage is not None:
        dbg_d = nc.dram_tensor("dbg", [128, NCH * T], F32, kind="ExternalOutput").ap()

    with ExitStack() as es:
        P = Prog(nc, es)

        def sb(name, shape, dt):
            return es.enter_context(nc.sbuf_tensor(name, shape, dt))

        hT = sb("hT", [128, NCH, T], F32)
        ybuf = sb("ybuf", [128, 8192], F32)
        kvbuf = sb("kvbuf", [128, 8192], F32)
        NWS = 4
        wbufs = [sb(f"wbuf{i}", [128, 2048], BF16) for i in range(NWS)]
        vecs = sb("vecs_sb", [128, NV], F32)
        ident = sb("ident_sb", [128, 128], F32)
        identb = sb("identb", [128, 128], BF16)
        onesb = sb("onesb", [128, 128], BF16)
        masks = sb("masks", [128, 3, 128], BF16)
        pswap = sb("pswap", [32, 32], F32)
        wdwb = sb("wdwb", [128, 16, CW], BF16)
        halo = sb("halo", [128, NCH, HALO], BF16)
        rs = [sb(f"rs{i}", [128, TT], F32) for i in range(2)]
        sq = [sb(f"sq{i}", [128, TT], BF16) for i in range(3)]
        XBYTES = nc.sbuf_bytes_remaining - 256
        XN = (XBYTES // 4) // 64 * 64
        xbuf = sb("xbuf", [128, XN], F32)
        assert XN * 4 >= 50 * 1024, XN

        yT = ybuf[:].bitcast(BF16).rearrange("p (c t) -> p c t", c=NCH)
        oacc = ybuf[:, 0:4096].rearrange("p (g t) -> p g t", g=4)
        lacc = ybuf[:, 4096:8192].rearrange("p (g t) -> p g t", g=4)
        kvb = kvbuf[:].bitcast(BF16)

        def KT(p):
            return kvb[:, p * 8192:p * 8192 + 4096].rearrange("p (h t) -> p h t", h=NKV)

        def VT(p):
            return kvb[:, p * 8192 + 4096:p * 8192 + 8192].rearrange("p (h t) -> p h t", h=NKV)

        vtile = kvb[:, 8192:16384].rearrange("p (c t) -> p c t", c=NCH)

        xoff = [0]

        def xcarve(nfloats):
            a = xoff[0]
            xoff[0] += nfloats
            assert xoff[0] <= XN, (xoff[0], XN)
            return xbuf[:, a:a + nfloats]

        def xreset():
            xoff[0] = 0

        xreset()
        uT = xcarve(NCH * UW // 2).bitcast(BF16).rearrange("p (c t) -> p c t", c=NCH)
        dgA = xcarve(16 * 128 // 2).bitcast(BF16).rearrange("p (k m) -> p k m", k=16)
        dgB = xcarve(15 * 128 // 2).bitcast(BF16).rearrange("p (k m) -> p k m", k=15)
        sgt = [xcarve(TT) for _ in range(2)]
        tmp_t = xcarve(TT)
        xstage = [xcarve(2048) for _ in range(0)]
        conv_end = xoff[0]
        xs_off = conv_end
        xreset()
        hid = [xcarve(8 * T // 2).bitcast(BF16).rearrange("p (c t) -> p c t", c=8) for _ in range(2)]
        stage = [xcarve(2048) for _ in range(2)]
        mlp_end = xoff[0]
        xreset()
        QT = xcarve(NHEAD * T // 2).bitcast(BF16).rearrange("p (h t) -> p h t", h=NHEAD)
        ropet = xcarve(2 * T)
        ropev = ropet.rearrange("p (a t) -> p a t", a=2)
        Pt = [xcarve(256).bitcast(BF16) for _ in range(4)]
        Vt = [xcarve(64).bitcast(BF16) for _ in range(6)]
        q32 = [xcarve(TT) for _ in range(1)]
        t2b = [xcarve(TT) for _ in range(1)]
        att_end = xoff[0]
        assert max(conv_end, mlp_end, att_end) <= XN

        XKEYS_CONV = [("u", c) for c in range(NCH)] + ["dgA", "dgB"] + [("sg", i) for i in range(2)] + \
                     ["tmp"]
        XKEYS_MLP = [("hid", i, j, tt) for i in range(2) for j in range(8) for tt in range(2)] + [("stage", i) for i in range(2)]
        XKEYS_ATT = [("Q", h) for h in range(NHEAD)] + ["ropet"] + [("Pt", i) for i in range(4)] + [("Vt", i) for i in range(6)] + \
                    [("q32", i) for i in range(1)] + [("t2", i) for i in range(1)]
        YKEYS = [("y", c, tt) for c in range(NCH) for tt in range(2)]

        psb = [es.enter_context(nc.psum_tensor(f"ps{i}", [128, TT], F32)) for i in range(7)]
        psbf = es.enter_context(nc.psum_tensor("psbf", [128, 1024], BF16))
        mmrot = Rot([0, 1, 2, 3])
        auxrot = Rot([4, 5, 6])
        bfbank = [psbf[:, 0:128], psb[3][:].bitcast(BF16)[:, 0:128]]
        bfkey = ["psbf", ("ps", 3)]
        bfrot = Rot([0, 1])
        attrot = Rot([0, 1, 2])

        def ts(tt):
            return slice(tt * TT, (tt + 1) * TT)

        def vcol(c):
            return vecs[:, c:c + 1]

        P.dma("sp", vecs[:], vecs_d, "c0", W=["vecs"])
        P.dma("sp", ident[:], cst_d[:, 0:128], "c1", W=["ident"])
        P.dma("sp", pswap[:], cst_d[0:32, 512:544], "c2", W=["pswap"])
        P.dma("pool", identb[:], cst_d[:, 0:128], "c3", W=["identb"])
        P.dma("pool", masks[:], cst_d[:, 128:512].rearrange("p (a b) -> p a b", a=3), "c4", W=["masks"])
        P.op("dve", lambda e: e.memset(onesb[:], 1.0), W=["ones"])
        P.op("dve", lambda e: e.tensor_copy(out=wdwb[:], in_=vecs[:, V_WDW:V_WDW + 16 * CW].rearrange("p (c k) -> p c k", c=16)),
             R=["vecs"], W=["wdwb"])
        P.op("dve", lambda e: e.memset(halo[:], 0.0), W=["halo"])

        wstate = {"i": 0}

        def wload(Wd, r0, nk, c0, ncols):
            s = wstate["i"] % NWS
            wstate["i"] += 1
            view = wbufs[s][:, 0:nk * ncols].rearrange("p (k c) -> p k c", k=nk)
            src = Wd[r0:r0 + nk * 128, c0:c0 + ncols].rearrange("(k p) c -> p k c", p=128)
            P.dma("pool", view, src, f"w{s}", W=[("w", s)])
            return s, view

        def mm_group(out_ap, pairs, R, W):
            def fn(e, out_ap=out_ap, pairs=pairs):
                n = len(pairs)
                ins = None
                for i, (l, r) in enumerate(pairs):
                    ins = e.matmul(out_ap, l, r, start=(i == 0), stop=(i == n - 1))
                return ins
            return P.op("pe", fn, R=R, W=W)

        def rmsnorm(gcol, out_f32_inplace=False):
            for tt in range(2):
                bank = auxrot.next()
                for c in range(NCH):
                    sqi = P_sq.next()
                    P.op("act", lambda e, c=c, sqi=sqi, tt=tt: e.activation(out=sq[sqi][:], in_=hT[:, c, ts(tt)], func=AF.Square),
                         R=[("h", c, tt)], W=[("sq", sqi)])
                    P.op("pe", lambda e, c=c, sqi=sqi, bank=bank: e.matmul(psb[bank][:], onesb[:], sq[sqi][:], start=(c == 0), stop=(c == NCH - 1)),
                         R=[("sq", sqi), "ones"], W=[("ps", bank)])
                ri = P_rs.next()
                P.op("act", lambda e, ri=ri, bank=bank: e.activation(out=rs[ri][:], in_=psb[bank][:], func=AF.Sqrt, scale=1.0 / D, bias=EPS_RMS),
                     R=[("ps", bank)], W=[("rs", ri)])
                P.op("dve", lambda e, ri=ri: e.reciprocal(out=rs[ri][:], in_=rs[ri][:]), R=[("rs", ri)], W=[("rs", ri)])
                for c in range(NCH):
                    if out_f32_inplace:
                        P.op("dve", lambda e, c=c, tt=tt, ri=ri: e.scalar_tensor_tensor(
                            out=hT[:, c, ts(tt)], in0=hT[:, c, ts(tt)], scalar=vcol(gcol + c), in1=rs[ri][:], op0=ALU.mult, op1=ALU.mult),
                            R=[("h", c, tt), ("rs", ri), "vecs"], W=[("h", c, tt)])
                    else:
                        P.op("dve", lambda e, c=c, tt=tt, ri=ri: e.scalar_tensor_tensor(
                            out=yT[:, c, ts(tt)], in0=hT[:, c, ts(tt)], scalar=vcol(gcol + c), in1=rs[ri][:], op0=ALU.mult, op1=ALU.mult),
                            R=[("h", c, tt), ("rs", ri), "vecs"], W=[("y", c, tt)])

        P_sq = Rot([0, 1, 2])
        P_rs = Rot([0, 1])

        def dense(Wd, r0, nk, col_tiles, ncols, rhs_fn, rkeys_fn, evac_fn):
            for c0 in col_tiles:
                s, view = wload(Wd, r0, nk, c0, ncols)
                for mi in range(ncols // 128):
                    for tt in range(2):
                        bank = mmrot.next()
                        pairs = [(view[:, k, mi * 128:(mi + 1) * 128], rhs_fn(k, tt)) for k in range(nk)]
                        R = [("w", s)]
                        for k in range(nk):
                            R.extend(rkeys_fn(k, tt))
                        mm_group(psb[bank][:], pairs, R, [("ps", bank)])
                        evac_fn((c0 // 128) + mi, tt, bank)

        def evac_add_h(bias_col=None):
            def f(m, tt, bank):
                if bias_col is None:
                    P.op("dve", lambda e: e.tensor_tensor(out=hT[:, m, ts(tt)], in0=psb[bank][:], in1=hT[:, m, ts(tt)], op=ALU.add),
                         R=[("ps", bank), ("h", m, tt)], W=[("h", m, tt)])
                else:
                    P.op("dve", lambda e: e.scalar_tensor_tensor(out=hT[:, m, ts(tt)], in0=psb[bank][:], scalar=vcol(bias_col + m),
                                                                 in1=hT[:, m, ts(tt)], op0=ALU.add, op1=ALU.add),
                         R=[("ps", bank), ("h", m, tt), "vecs"], W=[("h", m, tt)])
            return f

        def y_rhs(k, tt):
            return yT[:, k, ts(tt)]

        def y_keys(k, tt):
            return [("y", k, tt)]

        def mlp(layer):
            rmsnorm(V_NMLP0 if layer == 0 else V_NMLP1)
            Win, Wout = w_m_in[layer], w_m_out[layer]

            def w_in_group(G):
                hb = G % 2

                def evac(m, tt, bank):
                    j = m - G * 8
                    sqi = P_sq.next()
                    P.op("act", lambda e: e.activation(out=sq[sqi][:], in_=psb[bank][:], func=AF.Relu),
                         R=[("ps", bank)], W=[("sq", sqi)])
                    P.op("dve", lambda e: e.tensor_tensor(out=hid[hb][:, j, ts(tt)], in0=sq[sqi][:], in1=sq[sqi][:], op=ALU.mult),
                         R=[("sq", sqi)], W=[("hid", hb, j, tt)])
                dense(Win, 0, NCH, [(G * 8 + j) * 128 for j in range(8)], 128, y_rhs, y_keys, evac)

            def w_out_group(G):
                hb = G % 2
                dense(Wout, G * FG, 8, [c * 256 for c in range(8)], 256,
                      lambda k, tt: hid[hb][:, k, ts(tt)], lambda k, tt: [("hid", hb, k, tt)], evac_add_h())

            w_in_group(0)
            for G in range(NFG):
                if G + 1 < NFG:
                    w_in_group(G + 1)
                w_out_group(G)

        def rope_evac(bank, dst, tok0, dst_keys):
            qi = P_q32.next()
            P.op("act", lambda e: e.activation(out=dst, in_=psb[bank][:], func=AF.Copy),
                 R=[("ps", bank)], W=dst_keys)
            P.op("act", lambda e: e.activation(out=q32[qi][0:32, :], in_=psb[bank][0:32, :], func=AF.Copy),
                 R=[("ps", bank)], W=[("q32", qi)])
            b2 = auxrot.next()
            P.op("pe", lambda e: e.matmul(psb[b2][0:32, :], pswap[:], q32[qi][0:32, :], start=True, stop=True),
                 R=[("q32", qi), "pswap"], W=[("ps", b2)])
            P.op("dve", lambda e: e.tensor_tensor(out=q32[qi][0:32, :], in0=q32[qi][0:32, :], in1=ropev[0:32, 0, tok0:tok0 + TT], op=ALU.mult),
                 R=[("q32", qi), "ropet"], W=[("q32", qi)])
            P.op("dve", lambda e: e.tensor_tensor(out=t2b[qi][0:32, :], in0=psb[b2][0:32, :], in1=ropev[0:32, 1, tok0:tok0 + TT], op=ALU.mult),
                 R=[("ps", b2), "ropet"], W=[("t2", qi)])
            P.op("dve", lambda e: e.tensor_tensor(out=dst[0:32, :], in0=q32[qi][0:32, :], in1=t2b[qi][0:32, :], op=ALU.add),
                 R=[("q32", qi), ("t2", qi)], W=dst_keys)

        P_q32 = Rot([0])

        def dump_h(p):
            P.dma("sp", dbg_d, hT[:].rearrange("p c t -> p (c t)"), "dbg",
                  R=[("h", c, tt) for c in range(NCH) for tt in range(2)], W=["dbgout"])
            P.wait_all("sp", ["dbgout"])

        def dump_y(p):
            for c in range(NCH):
                for tt in range(2):
                    P.op("act", lambda e, c=c, tt=tt: e.activation(out=hT[:, c, ts(tt)], in_=yT[:, c, ts(tt)], func=AF.Copy),
                         R=[("y", c, tt)], W=[("h", c, tt)])
            dump_h(p)

        for p in range(NPASS):
            tokp = p * T
            P.alias([("stage", i) for i in range(2)], XKEYS_CONV + XKEYS_ATT + XKEYS_MLP)
            for b in range(T // 128):
                si = b % 2
                P.dma("sp", stage[si], x_d[tokp + b * 128:tokp + (b + 1) * 128, :], f"xs{si}", W=[("stage", si)])
                for cg in range(4):
                    bank = auxrot.next()

                    def fn(e, si=si, cg=cg, bank=bank):
                        ins = None
                        for j in range(4):
                            c = cg * 4 + j
                            ins = e.transpose(psb[bank][:, j * 128:(j + 1) * 128], stage[si][:, c * 128:(c + 1) * 128], ident[:])
                        return ins
                    P.op("pe", fn, R=[("stage", si), "ident"], W=[("ps", bank)])
                    tt = (b * 128) // TT
                    P.op("act", lambda e, cg=cg, bank=bank, b=b: e.activation(
                        out=hT[:, cg * 4:cg * 4 + 4, b * 128:(b + 1) * 128],
                        in_=psb[bank][:].rearrange("p (j t) -> p j t", j=4), func=AF.Copy),
                        R=[("ps", bank)], W=[("h", cg * 4 + j, tt) for j in range(4)])
            if debug_stage == "load" and p == 0:
                dump_h(p)
                break

            rmsnorm(V_NMIX0)
            if debug_stage == "norm0" and p == 0:
                dump_y(p)
                break
            P.alias(XKEYS_CONV, XKEYS_CONV + XKEYS_ATT + XKEYS_MLP)
            P.op("dve", lambda e: e.tensor_copy(out=uT[:, :, 0:HALO], in_=halo[:]), R=["halo"], W=[("u", c) for c in range(NCH)])
            for j in range(NCH):
                sg_, gview = wload(w_cin, 0, NCH, D + j * 128, 128)
                sa_, aview = wload(w_cin, 0, NCH, j * 128, 128)
                for tt in range(2):
                    bank = mmrot.next()
                    mm_group(psb[bank][:], [(gview[:, k, :], yT[:, k, ts(tt)]) for k in range(NCH)],
                             [("w", sg_)] + [("y", k, tt) for k in range(NCH)], [("ps", bank)])
                    si = P_sg.next() if False else (j * 2 + tt) % 2
                    P.op("act", lambda e, bank=bank, si=si, j=j: e.activation(out=sgt[si], in_=psb[bank][:], func=AF.Sigmoid,
                                                                               bias=vcol(V_BIN + 16 + j)),
                         R=[("ps", bank), "vecs"], W=[("sg", si)])
                    bank2 = mmrot.next()
                    mm_group(psb[bank2][:], [(aview[:, k, :], yT[:, k, ts(tt)]) for k in range(NCH)],
                             [("w", sa_)] + [("y", k, tt) for k in range(NCH)], [("ps", bank2)])
                    P.op("dve", lambda e, bank2=bank2, si=si, j=j, tt=tt: e.scalar_tensor_tensor(
                        out=uT[:, j, HALO + tt * TT:HALO + (tt + 1) * TT], in0=psb[bank2][:], scalar=vcol(V_BIN + j),
                        in1=sgt[si], op0=ALU.add, op1=ALU.mult),
                        R=[("ps", bank2), ("sg", si), "vecs"], W=[("u", j)])
            P.op("dve", lambda e: e.tensor_copy(out=halo[:], in_=uT[:, :, T:T + HALO]), R=[("u", c) for c in range(NCH)], W=["halo"])
            if debug_stage == "glu" and p == 0:
                for c in range(NCH):
                    for tt in range(2):
                        P.op("act", lambda e, c=c, tt=tt: e.activation(out=hT[:, c, ts(tt)], in_=uT[:, c, HALO + tt * TT:HALO + (tt + 1) * TT], func=AF.Copy),
                             R=[("u", c)], W=[("h", c, tt)])
                dump_h(p)
                break

            VKEYS = [("v", c) for c in range(NCH)]
            P.alias(VKEYS, [("KT", h, 1) for h in range(NKV)] + [("VT", h, 1) for h in range(NKV)] + VKEYS)
            def build_diag(c):
                P.op("dve", lambda e, c=c: e.tensor_tensor(
                    out=dgA, in0=identb[:].unsqueeze(1).broadcast_to([128, 16, 128]),
                    in1=wdwb[:, c, 0:16].unsqueeze(2).broadcast_to([128, 16, 128]), op=ALU.mult),
                    R=["identb", "wdwb"], W=["dgA"])
                P.op("dve", lambda e, c=c: e.tensor_tensor(
                    out=dgB, in0=identb[:].unsqueeze(1).broadcast_to([128, 15, 128]),
                    in1=wdwb[:, c, 16:CW].unsqueeze(2).broadcast_to([128, 15, 128]), op=ALU.mult),
                    R=["identb", "wdwb"], W=["dgB"])

            for q in range(T // TT):
                tq = q * TT
                b1 = auxrot.next()
                b2 = auxrot.next()
                for c in range(NCH):
                    if q == 0 and c == 0:
                        build_diag(0)
                    bank = mmrot.next()

                    def fnA(e, c=c, bank=bank, tq=tq):
                        ins = None
                        for k in range(16):
                            ins = e.matmul(psb[bank][:], dgA[:, k, :], uT[:, c, tq + k:tq + k + TT], start=(k == 0), stop=False)
                        return ins

                    def fnB(e, c=c, bank=bank, tq=tq):
                        ins = None
                        for k in range(16, CW):
                            ins = e.matmul(psb[bank][:], dgB[:, k - 16, :], uT[:, c, tq + k:tq + k + TT], start=False, stop=(k == CW - 1))
                        return ins
                    P.op("pe", fnA, R=["dgA", ("u", c)], W=[("ps", bank)])
                    P.op("pe", fnB, R=["dgB", ("u", c)], W=[("ps", bank)])
                    if not (q == T // TT - 1 and c == NCH - 1):
                        build_diag((c + 1) % NCH)
                    P.op("act", lambda e, c=c, bank=bank: e.activation(out=vtile[:, c, :], in_=psb[bank][:], func=AF.Identity,
                                                                        bias=vcol(V_BDW + c)),
                         R=[("ps", bank), "vecs"], W=[("v", c)])
                    sqi = P_sq.next()
                    P.op("act", lambda e, c=c, bank=bank, sqi=sqi: e.activation(out=sq[sqi][:], in_=psb[bank][:], func=AF.Square,
                                                                                 bias=vcol(V_BDW + c)),
                         R=[("ps", bank), "vecs"], W=[("sq", sqi)])
                    P.op("pe", lambda e, c=c, b1=b1: e.matmul(psb[b1][:], onesb[:], vtile[:, c, :], start=(c == 0), stop=(c == NCH - 1)),
                         R=[("v", c), "ones"], W=[("ps", b1)])
                    P.op("pe", lambda e, c=c, sqi=sqi, b2=b2: e.matmul(psb[b2][:], onesb[:], sq[sqi][:], start=(c == 0), stop=(c == NCH - 1)),
                         R=[("sq", sqi), "ones"], W=[("ps", b2)])
                P.op("act", lambda e, b1=b1: e.activation(out=rs[0][:], in_=psb[b1][:], func=AF.Copy, scale=1.0 / D),
                     R=[("ps", b1)], W=[("rs", 0)])
                P.op("dve", lambda e: e.tensor_tensor(out=tmp_t, in0=rs[0][:], in1=rs[0][:], op=ALU.mult), R=[("rs", 0)], W=["tmp"])
                P.op("dve", lambda e, b2=b2: e.scalar_tensor_tensor(out=rs[1][:], in0=psb[b2][:], scalar=1.0 / D, in1=tmp_t,
                                                                    op0=ALU.mult, op1=ALU.subtract),
                     R=[("ps", b2), "tmp"], W=[("rs", 1)])
                P.op("act", lambda e: e.activation(out=rs[1][:], in_=rs[1][:], func=AF.Sqrt, bias=EPS_LN), R=[("rs", 1)], W=[("rs", 1)])
                P.op("dve", lambda e: e.reciprocal(out=rs[1][:], in_=rs[1][:]), R=[("rs", 1)], W=[("rs", 1)])
                for c in range(NCH):
                    si = c % 2
                    P.op("dve", lambda e, c=c, si=si: e.tensor_tensor(out=sgt[si], in0=vtile[:, c, :], in1=rs[0][:], op=ALU.subtract),
                         R=[("v", c), ("rs", 0)], W=[("sg", si)])
                    P.op("dve", lambda e, si=si: e.tensor_tensor(out=sgt[si], in0=sgt[si], in1=rs[1][:], op=ALU.mult),
                         R=[("sg", si), ("rs", 1)], W=[("sg", si)])
                    P.op("act", lambda e, c=c, tq=tq, si=si: e.activation(out=yT[:, c, tq:tq + TT], in_=sgt[si], func=AF.Silu,
                                                                           scale=vcol(V_LNG + c), bias=vcol(V_LNB + c)),
                         R=[("sg", si), "vecs"], W=[("y", c, q)])
            if debug_stage == "conv" and p == 0:
                dump_y(p)
                break
            dense(w_cout, 0, NCH, [m * 128 for m in range(NCH)], 128, y_rhs, y_keys, evac_add_h(V_BOUT))
            if debug_stage == "mix0" and p == 0:
                dump_h(p)
                break

            P.alias(XKEYS_MLP, XKEYS_CONV + XKEYS_ATT + XKEYS_MLP)
            mlp(0)
            if debug_stage == "mlp0" and p == 0:
                dump_h(p)
                break

            P.alias(XKEYS_ATT, XKEYS_CONV + XKEYS_ATT + XKEYS_MLP)
            P.alias([("KT", h, 1) for h in range(NKV)] + [("VT", h, 1) for h in range(NKV)],
                    [("KT", h, 1) for h in range(NKV)] + [("VT", h, 1) for h in range(NKV)] + VKEYS)
            P.dma("sp", ropev[0:32, :, :], rope_d[:, :, tokp:tokp + T], "rope", W=["ropet"])
            rmsnorm(V_KVN)
            for h in range(NKV):
                dense(w_kv, 0, NCH, [h * 128], 128, y_rhs, y_keys,
                      lambda m, tt, bank, h=h: rope_evac(bank, KT(p)[:, h, ts(tt)], tt * TT, [("KT", h, p)]))
            for h in range(NKV):
                def evac_v(m, tt, bank, h=h, p=p):
                    dstv = VT(p)[:, h, ts(tt)]
                    P.op("act", lambda e: e.activation(out=dstv, in_=psb[bank][:], func=AF.Copy),
                         R=[("ps", bank)], W=[("VT", h, p)])
                dense(w_kv, 0, NCH, [512 + h * 128], 128, y_rhs, y_keys, evac_v)

            rmsnorm(V_NMIX1)
            for hd in range(NHEAD):
                dense(w_q, 0, NCH, [hd * 128], 128, y_rhs, y_keys,
                      lambda m, tt, bank, hd=hd: rope_evac(bank, QT[:, hd, ts(tt)], tt * TT, [("Q", hd)]))
            if debug_stage == "q" and p == 0:
                for c in range(NCH):
                    for tt in range(2):
                        P.op("act", lambda e, c=c, tt=tt: e.activation(out=hT[:, c, ts(tt)], in_=QT[:, c, ts(tt)], func=AF.Copy),
                             R=[("Q", c)], W=[("h", c, tt)])
                dump_h(p)
                break

            if debug_stage == "kvcmp" and p == 1:
                P.dma("sp", dbg_d[:, 8192:16384], kvbuf[:], "dbg",
                      R=[("KT", h, pp) for h in range(NKV) for pp in range(2)] + [("VT", h, pp) for h in range(NKV) for pp in range(2)], W=["dbgout"])
                P.wait_all("sp", ["dbgout"])
                break
            P.alias(["oacc", "lacc"], YKEYS + ["oacc", "lacc"])
            vt_cache = {}

            def kt_ap(h, g0, r, n):
                pp = g0 // T
                l0 = g0 - pp * T
                assert l0 + r * (n - 1) < T
                return KT(pp)[:, h, l0:l0 + r * (n - 1) + 1:r], VT(pp)[:, h, l0:l0 + r * (n - 1) + 1:r], pp

            for h in range(NKV):
                work = []
                for n in range(T // 128):
                    N = p * 8 + n
                    kts = []
                    if N >= 1:
                        kts.append(((N - 1) * 128, 128, 1))
                    kts.append((N * 128, 128, 0))
                    work.append((1, n * 128, 128, kts))
                for rho in range(4):
                    for nn in range(2):
                        N4 = 2 * p + nn
                        kts = []
                        if N4 >= 1:
                            kts.append((512 * (N4 - 1) + rho, 128, 1))
                        kts.append((512 * N4 + rho, 128, 0))
                        work.append((4, 512 * N4 + rho - tokp, 128, kts))
                for rho in range(16):
                    if p == 0:
                        work.append((16, rho, 64, [(rho, 64, 0)]))
                    else:
                        work.append((16, rho, 64, [(rho, 64, 3), (T + rho, 64, 0)]))
                def stage_a(item, h=h):
                    (r, qs, qn, kts) = item
                    ncol = 4 * qn
                    qap = QT[:, 4 * h:4 * h + 4, qs:qs + r * (qn - 1) + 1:r]
                    tiles = []
                    for ki, (g0, kn, mid) in enumerate(kts):
                        kap, vap, pp = kt_ap(h, g0, r, kn)
                        bs = attrot.next()
                        P.op("pe", lambda e, bs=bs, kap=kap, qap=qap, kn=kn, ncol=ncol, qn=qn: e.matmul(
                            psb[bs][0:kn, 0:ncol].rearrange("p (g q) -> p g q", g=4), kap, qap, start=True, stop=True),
                            R=[("KT", h, pp)] + [("Q", 4 * h + g) for g in range(4)], W=[("ps", bs)])
                        pi = P_pt.next()
                        P.op("act", lambda e, bs=bs, pi=pi, kn=kn, ncol=ncol: e.activation(out=Pt[pi][0:kn, 0:ncol], in_=psb[bs][0:kn, 0:ncol],
                                                                                          func=AF.Exp, scale=SCALE),
                             R=[("ps", bs)], W=[("Pt", pi)])
                        if mid != 3:
                            P.op("dve", lambda e, pi=pi, kn=kn, ncol=ncol, qn=qn, mid=mid: e.tensor_tensor(
                                out=Pt[pi][0:kn, 0:ncol].rearrange("p (g q) -> p g q", g=4),
                                in0=Pt[pi][0:kn, 0:ncol].rearrange("p (g q) -> p g q", g=4),
                                in1=masks[0:kn, mid, 0:qn].unsqueeze(1).broadcast_to([kn, 4, qn]), op=ALU.mult),
                                R=[("Pt", pi), "masks"], W=[("Pt", pi)])
                        ck = (h, g0, r, kn)
                        if ck in vt_cache and vt_cache[ck][1] == P_vt.i:
                            vi = vt_cache[ck][0]
                        else:
                            vi = P_vt.next()
                            vt_cache[ck] = (vi, P_vt.i)
                            qb = bfrot.next()
                            P.op("pe", lambda e, qb=qb, vap=vap, kn=kn: e.transpose(bfbank[qb][0:kn, :], vap, identb[:]),
                                 R=[("VT", h, pp), "identb"], W=[bfkey[qb]])
                            P.op("act", lambda e, qb=qb, vi=vi, kn=kn: e.activation(out=Vt[vi][0:kn, :], in_=bfbank[qb][0:kn, :], func=AF.Copy),
                                 R=[bfkey[qb]], W=[("Vt", vi)])
                        tiles.append((pi, vi, kn))
                    return tiles

                def stage_b(item, tiles, h=h):
                    (r, qs, qn, kts) = item
                    ncol = 4 * qn
                    bo = auxrot.next()
                    bl = auxrot.next()
                    nt = len(tiles)
                    for ki, (pi, vi, kn) in enumerate(tiles):
                        last = (ki == nt - 1)
                        P.op("pe", lambda e, bo=bo, vi=vi, pi=pi, kn=kn, ncol=ncol, ki=ki, last=last: e.matmul(
                            psb[bo][:, 0:ncol], Vt[vi][0:kn, :], Pt[pi][0:kn, 0:ncol], start=(ki == 0), stop=last),
                            R=[("Vt", vi), ("Pt", pi)], W=[("ps", bo)])
                        P.op("pe", lambda e, bl=bl, pi=pi, kn=kn, ncol=ncol, ki=ki, last=last: e.matmul(
                            psb[bl][:, 0:ncol], onesb[0:kn, :], Pt[pi][0:kn, 0:ncol], start=(ki == 0), stop=last),
                            R=[("Pt", pi), "ones"], W=[("ps", bl)])
                    oap = oacc[:, :, qs:qs + r * (qn - 1) + 1:r]
                    lap = lacc[:, :, qs:qs + r * (qn - 1) + 1:r]
                    ops_ = psb[bo][:, 0:ncol].rearrange("p (g q) -> p g q", g=4)
                    lps_ = psb[bl][:, 0:ncol].rearrange("p (g q) -> p g q", g=4)
                    if r == 1:
                        P.op("act", lambda e, oap=oap, ops_=ops_: e.activation(out=oap, in_=ops_, func=AF.Copy),
                             R=[("ps", bo)], W=["oacc"])
                        P.op("act", lambda e, lap=lap, lps_=lps_: e.activation(out=lap, in_=lps_, func=AF.Copy),
                             R=[("ps", bl)], W=["lacc"])
                    else:
                        P.op("dve", lambda e, oap=oap, ops_=ops_: e.tensor_tensor(out=oap, in0=ops_, in1=oap, op=ALU.add),
                             R=[("ps", bo), "oacc"], W=["oacc"])
                        P.op("dve", lambda e, lap=lap, lps_=lps_: e.tensor_tensor(out=lap, in0=lps_, in1=lap, op=ALU.add),
                             R=[("ps", bl), "lacc"], W=["lacc"])

                pending = []
                for item in work:
                    tl = stage_a(item)
                    pending.append((item, tl))
                    if len(pending) > 1:
                        stage_b(*pending.pop(0))
                while pending:
                    stage_b(*pending.pop(0))
                P.op("dve", lambda e: e.reciprocal(out=lacc, in_=lacc), R=["lacc"], W=["lacc"])
                for g in range(4):
                    P.op("dve", lambda e, g=g, h=h: e.tensor_tensor(out=QT[:, 4 * h + g, :], in0=oacc[:, g, :], in1=lacc[:, g, :], op=ALU.mult),
                         R=["oacc", "lacc"], W=[("Q", 4 * h + g)])
            P.alias(YKEYS, YKEYS + ["oacc", "lacc"])
            if debug_stage == "kvcmp" and p == 0:
                P.dma("sp", dbg_d[:, 0:8192], kvbuf[:], "dbg",
                      R=[("KT", h, pp) for h in range(NKV) for pp in range(1)] + [("VT", h, pp) for h in range(NKV) for pp in range(1)], W=["dbgout"])
            if debug_stage == "attn" and p == 0:
                for c in range(NCH):
                    for tt in range(2):
                        P.op("act", lambda e, c=c, tt=tt: e.activation(out=hT[:, c, ts(tt)], in_=QT[:, c, ts(tt)], func=AF.Copy),
                             R=[("Q", c)], W=[("h", c, tt)])
                dump_h(p)
                break
            dense(w_o, 0, NCH, [m * 128 for m in range(NCH)], 128,
                  lambda k, tt: QT[:, k, ts(tt)], lambda k, tt: [("Q", k)], evac_add_h())
            if debug_stage == "mix1" and p == 0:
                dump_h(p)
                break

            P.alias(XKEYS_MLP, XKEYS_CONV + XKEYS_ATT + XKEYS_MLP)
            mlp(1)
            if debug_stage == "mlp1" and p == 0:
                dump_h(p)
                break

            rmsnorm(V_FIN, out_f32_inplace=True)
            for b in range(T // 128):
                si = b % 2
                tt = (b * 128) // TT
                for cg in range(4):
                    bank = auxrot.next()

                    def fn(e, cg=cg, bank=bank, b=b):
                        ins = None
                        for j in range(4):
                            c = cg * 4 + j
                            ins = e.transpose(psb[bank][:, j * 128:(j + 1) * 128], hT[:, c, b * 128:(b + 1) * 128], ident[:])
                        return ins
                    P.op("pe", fn, R=[("h", cg * 4 + j, tt) for j in range(4)] + ["ident"], W=[("ps", bank)])
                    P.op("act", lambda e, cg=cg, bank=bank, si=si: e.activation(out=stage[si][:, cg * 512:(cg + 1) * 512], in_=psb[bank][:], func=AF.Copy),
                         R=[("ps", bank)], W=[("stage", si)])
                P.dma("sp", out_d[tokp + b * 128:tokp + (b + 1) * 128, :], stage[si], f"os{si}", R=[("stage", si)], W=[("outrow", b % 2)])
        P.wait_all("sp", [("outrow", 0), ("outrow", 1)])

        block = es.enter_context(nc.Block())

        @block.tensor
        def _(e):
            P.emit("pe", e)

        @block.scalar
        def _(e):
            P.emit("act", e)

        @block.vector
        def _(e):
            P.emit("dve", e)

        @block.gpsimd
        def _(e):
            P.emit("pool", e)

        @block.sync
        def _(e):
            P.emit("sp", e)
    return nc


P_sg = None
P_pt = Rot([0, 1, 2, 3])
P_vt = Rot([0, 1, 2, 3, 4, 5])
P_vb = Rot([0, 1, 2])


def _reset_rots():
    global P_pt, P_vt, P_vb
    P_pt = Rot([0, 1, 2, 3])
    P_vt = Rot([0, 1, 2, 3, 4, 5])
    P_vb = Rot([0, 1, 2])


def host_consts():
    half = 16
    pos = np.arange(S, dtype=np.float32)
    inv = (np.float32(ROPE_THETA) ** (-np.arange(0, 32, 2, dtype=np.float32) / np.float32(32))).astype(np.float32)
    ang = (pos[:, None] * inv[None, :]).astype(np.float32)
    cos = np.cos(ang).astype(np.float32).T
    sin = np.sin(ang).astype(np.float32).T
    rope = np.zeros((32, 2, S), np.float32)
    rope[0:16, 0] = cos
    rope[16:32, 0] = cos
    rope[0:16, 1] = -sin
    rope[16:32, 1] = sin
    cst = np.zeros((128, 128 + 3 * 128 + 32), np.float32)
    cst[:, 0:128] = np.eye(128, dtype=np.float32)
    k = np.arange(128)[:, None]
    q = np.arange(128)[None, :]
    cst[:, 128:256] = (k <= q)
    cst[:, 256:384] = (k >= q)
    cst[:, 384:512] = 1.0
    for m in range(32):
        cst[(m + 16) % 32, 512 + m] = 1.0
    return rope, cst


def pack_vecs(inp):
    v = np.zeros((128, NV), np.float32)

    def put(col, vec):
        n = vec.shape[0] // 128
        v[:, col:col + n] = vec.reshape(n, 128).T

    put(V_NMIX0, inp["norm_mix"][0])
    put(V_NMIX1, inp["norm_mix"][1])
    put(V_NMLP0, inp["norm_mlp"][0])
    put(V_NMLP1, inp["norm_mlp"][1])
    put(V_BIN, inp["conv_b_in"][0])
    put(V_BDW, inp["conv_b_dw"][0])
    put(V_LNG, inp["conv_ln_g"][0])
    put(V_LNB, inp["conv_ln_b"][0])
    put(V_BOUT, inp["conv_b_out"][0])
    put(V_KVN, inp["kv_norm"])
    put(V_FIN, inp["final_norm"])
    wdw = inp["conv_w_dw"][0]
    v[:, V_WDW:] = wdw.T.reshape(16, 128, CW).transpose(1, 0, 2).reshape(128, 16 * CW)
    return v


def make_in_map(inp, xi, rope, cst, vecs):
    f = lambda a: np.ascontiguousarray(a, dtype=np.float32)
    return {
        "x": f(xi), "vecs": vecs, "rope": rope, "cst": cst,
        "conv_w_in": f(inp["conv_w_in"][0]), "conv_w_out": f(inp["conv_w_out"][0]),
        "w_kv": f(inp["w_kv"]), "attn_w_q": f(inp["attn_w_q"][0]), "attn_w_o": f(inp["attn_w_o"][0]),
        "mlp_w_in0": f(inp["mlp_w_in"][0]), "mlp_w_in1": f(inp["mlp_w_in"][1]),
        "mlp_w_out0": f(inp["mlp_w_out"][0]), "mlp_w_out1": f(inp["mlp_w_out"][1]),
    }


def kernel(**inputs):
    inp = {k: np.asarray(v) for k, v in inputs.items()}
    _reset_rots()
    nc = build_nc()
    rope, cst = host_consts()
    vecs = pack_vecs(inp)
    n = 8
    in_maps = [make_in_map(inp, inp["x"][i], rope, cst, vecs) for i in range(n)]
    res = run_bass_kernel_spmd(nc, in_maps, core_ids=list(range(n)))
    out = np.stack([np.asarray(res.results[i]["out"], dtype=np.float32).reshape(S, D) for i in range(n)], axis=0)
    return out
```
